# Optimizing a Trainium2 kernel written in Bass

```python
import math
import jax, jax.numpy as jnp
from jax import lax
import numpy as np


D_MODEL = 1024
BATCH = 2
SEQ = 16384
DEPTH = 2

N_A_LAYERS = DEPTH // 2
N_B_LAYERS = DEPTH - N_A_LAYERS
D_FF = 2816
GM_HALF = D_MODEL
GM_GROUPS = 8
GM_GROUP_CH = GM_HALF // GM_GROUPS
CHUNK = 128
DA_HEADS = 8
DA_HEAD_DIM = D_MODEL // (2 * DA_HEADS)
DA_WIDTH = DA_HEADS * 2 * DA_HEAD_DIM
Q_BLOCK = 128
RMS_EPS = 1e-6
LN_EPS = 1e-5

kernel_name = "yoco_gmlp_diffattn_macaron"


def rms_norm(x, g):
    xf = x.astype(jnp.float32)
    y = xf * lax.rsqrt(jnp.mean(xf * xf, axis=-1, keepdims=True) + RMS_EPS)
    return (y * g.astype(jnp.float32)).astype(x.dtype)


def swiglu(h, w_gate, w_up, w_down):
    return (jax.nn.silu(h @ w_gate) * (h @ w_up)) @ w_down


def lambda_init_fn(layer_idx):
    return 0.8 - 0.6 * math.exp(-0.3 * layer_idx)


def chunked_gmlp(h, w_in, ln_g, ln_b, w_s, b_s, w_out):
    B, S, _ = h.shape
    z = jax.nn.gelu(h @ w_in, approximate=False)
    u, v = z[..., :GM_HALF], z[..., GM_HALF:]
    vf = v.astype(jnp.float32)
    mu = jnp.mean(vf, axis=-1, keepdims=True)
    var = jnp.mean(jnp.square(vf - mu), axis=-1, keepdims=True)
    v = ((vf - mu) * lax.rsqrt(var + LN_EPS) * ln_g.astype(jnp.float32)
         + ln_b.astype(jnp.float32)).astype(h.dtype)
    causal = jnp.tril(jnp.ones((CHUNK, CHUNK), dtype=bool))
    ws = jnp.where(causal[None], w_s, jnp.zeros_like(w_s))
    vc = v.reshape(B, S // CHUNK, CHUNK, GM_GROUPS, GM_GROUP_CH)
    sv = jnp.einsum('gts,bnsgc->bntgc', ws, vc) + b_s.T[None, None, :, :, None]
    return (u * sv.reshape(B, S, GM_HALF)) @ w_out


def shared_kv(x, g, w_k, w_v):
    h = rms_norm(x, g)
    B, S, _ = h.shape
    k = (h @ w_k).reshape(B, S, DA_HEADS, 2, DA_HEAD_DIM).transpose(0, 2, 3, 1, 4)
    v = (h @ w_v).reshape(B, S, DA_HEADS, 2 * DA_HEAD_DIM).transpose(0, 2, 1, 3)
    return k, v


def diff_attention(h, k_sh, v_sh, w_q, lam_q1, lam_k1, lam_q2, lam_k2, subln_g, w_o, lambda_init):
    B, S, _ = h.shape
    nb = S // Q_BLOCK
    q = (h @ w_q).reshape(B, nb, Q_BLOCK, DA_HEADS, 2, DA_HEAD_DIM)
    q_blocks = q.transpose(1, 0, 3, 4, 2, 5)
    lam = (jnp.exp(jnp.sum(lam_q1.astype(jnp.float32) * lam_k1.astype(jnp.float32)))
           - jnp.exp(jnp.sum(lam_q2.astype(jnp.float32) * lam_k2.astype(jnp.float32)))
           + lambda_init)
    slopes = jnp.exp2(-8.0 * (jnp.arange(DA_HEADS, dtype=jnp.float32) + 1.0) / DA_HEADS)
    key_pos = jnp.arange(S, dtype=jnp.int32)
    scale = DA_HEAD_DIM ** -0.5

    def one_block(args):
        qb, start = args
        s = jnp.einsum('bhiqd,bhikd->bhiqk', qb, k_sh).astype(jnp.float32) * scale
        dist = (start + jnp.arange(Q_BLOCK, dtype=jnp.int32))[:, None] - key_pos[None, :]
        s = s - slopes[None, :, None, None, None] * dist.astype(jnp.float32)
        s = jnp.where(dist >= 0, s, -jnp.inf)
        p = jax.nn.softmax(s, axis=-1)
        a = p[:, :, 0] - lam * p[:, :, 1]
        return jnp.einsum('bhqk,bhkd->bhqd', a.astype(v_sh.dtype), v_sh)

    starts = jnp.arange(nb, dtype=jnp.int32) * Q_BLOCK
    o = lax.map(one_block, (q_blocks, starts))
    o = o.transpose(1, 0, 3, 2, 4).reshape(B, S, DA_HEADS, 2 * DA_HEAD_DIM)
    o = rms_norm(o, subln_g) * (1.0 - lambda_init)
    return o.reshape(B, S, DA_WIDTH) @ w_o


def setup_inputs(seed: int = 0) -> dict:
    key = jax.random.key(seed)
    ks = iter(jax.random.split(key, 40))

    def nrm(shape, scale):
        return jax.random.normal(next(ks), shape, jnp.float32) * scale

    def gain(shape):
        return 1.0 + nrm(shape, 0.02)

    D, F = D_MODEL, D_FF
    return {
        "x": nrm((BATCH, SEQ, D), 1.0),
        "ffn_norm1_g": gain((DEPTH, D)),
        "ffn1_w_gate": nrm((DEPTH, D, F), D ** -0.5),
        "ffn1_w_up": nrm((DEPTH, D, F), D ** -0.5),
        "ffn1_w_down": nrm((DEPTH, F, D), F ** -0.5),
        "mix_norm_g": gain((DEPTH, D)),
        "ffn_norm2_g": gain((DEPTH, D)),
        "ffn2_w_gate": nrm((DEPTH, D, F), D ** -0.5),
        "ffn2_w_up": nrm((DEPTH, D, F), D ** -0.5),
        "ffn2_w_down": nrm((DEPTH, F, D), F ** -0.5),
        "gm_w_in": nrm((N_A_LAYERS, D, 2 * GM_HALF), D ** -0.5),
        "gm_ln_g": gain((N_A_LAYERS, GM_HALF)),
        "gm_ln_b": nrm((N_A_LAYERS, GM_HALF), 0.02),
        "gm_w_s": nrm((N_A_LAYERS, GM_GROUPS, CHUNK, CHUNK), CHUNK ** -0.5),
        "gm_b_s": gain((N_A_LAYERS, GM_GROUPS, CHUNK)),
        "gm_w_out": nrm((N_A_LAYERS, GM_HALF, D), GM_HALF ** -0.5),
        "kv_norm_g": gain((D,)),
        "w_k": nrm((D, DA_WIDTH), D ** -0.5),
        "w_v": nrm((D, DA_WIDTH), D ** -0.5),
        "da_w_q": nrm((N_B_LAYERS, D, DA_WIDTH), D ** -0.5),
        "da_lam_q1": nrm((N_B_LAYERS, DA_HEAD_DIM), 0.1),
        "da_lam_k1": nrm((N_B_LAYERS, DA_HEAD_DIM), 0.1),
        "da_lam_q2": nrm((N_B_LAYERS, DA_HEAD_DIM), 0.1),
        "da_lam_k2": nrm((N_B_LAYERS, DA_HEAD_DIM), 0.1),
        "da_subln_g": gain((N_B_LAYERS, 2 * DA_HEAD_DIM)),
        "da_w_o": nrm((N_B_LAYERS, DA_WIDTH, D), DA_WIDTH ** -0.5),
        "final_norm_g": gain((D,)),
    }


def reference(x, ffn_norm1_g, ffn1_w_gate, ffn1_w_up, ffn1_w_down, mix_norm_g,
              ffn_norm2_g, ffn2_w_gate, ffn2_w_up, ffn2_w_down,
              gm_w_in, gm_ln_g, gm_ln_b, gm_w_s, gm_b_s, gm_w_out,
              kv_norm_g, w_k, w_v,
              da_w_q, da_lam_q1, da_lam_k1, da_lam_q2, da_lam_k2, da_subln_g, da_w_o,
              final_norm_g):
    k_sh = None
    v_sh = None
    for l in range(DEPTH):
        if l == N_A_LAYERS:
            k_sh, v_sh = shared_kv(x, kv_norm_g, w_k, w_v)
        x = x + 0.5 * swiglu(rms_norm(x, ffn_norm1_g[l]), ffn1_w_gate[l], ffn1_w_up[l], ffn1_w_down[l])
        h = rms_norm(x, mix_norm_g[l])
        if l < N_A_LAYERS:
            x = x + chunked_gmlp(h, gm_w_in[l], gm_ln_g[l], gm_ln_b[l], gm_w_s[l], gm_b_s[l], gm_w_out[l])
        else:
            j = l - N_A_LAYERS
            x = x + diff_attention(h, k_sh, v_sh, da_w_q[j], da_lam_q1[j], da_lam_k1[j],
                                   da_lam_q2[j], da_lam_k2[j], da_subln_g[j], da_w_o[j],
                                   lambda_init_fn(l))
        x = x + 0.5 * swiglu(rms_norm(x, ffn_norm2_g[l]), ffn2_w_gate[l], ffn2_w_up[l], ffn2_w_down[l])
    return rms_norm(x, final_norm_g)
```

```python
import math
from contextlib import ExitStack

import numpy as np
import ml_dtypes

import concourse.bass as bass
import concourse.mybir as mybir
from concourse.bass_utils import run_bass_kernel_spmd

F32 = mybir.dt.float32
BF16 = mybir.dt.bfloat16
ALU = mybir.AluOpType
AF = mybir.ActivationFunctionType

D = 1024
KD = 8
FF = 2816
FC = 22
NCORE = 8
TCORE = 4096
TT = 1024
NT = TCORE // TT
SEQ = 16384
RMS_EPS = 1e-6
LN_EPS = 1e-5
LAMBDA_INIT = 0.8 - 0.6 * math.exp(-0.3 * 1)
NDELTA = 131

ENGS = ["pe", "act", "dve", "pool", "sp"]
EPOCH = 12000


class Sched:
    def __init__(self):
        self.ops = []
        self.lastw = {}
        self.readers = {}
        self.last_of_eng = {}
        self.dma_ops = []

    def add(self, eng, fn, reads=(), writes=(), dma_sem=None, extra_deps=()):
        i = len(self.ops)
        deps = set(extra_deps)
        for r in reads:
            if r in self.lastw:
                deps.add(self.lastw[r])
        for w in writes:
            if w in self.lastw:
                deps.add(self.lastw[w])
            deps.update(self.readers.get(w, ()))
        for r in reads:
            self.readers.setdefault(r, []).append(i)
        for w in writes:
            self.lastw[w] = i
            self.readers[w] = []
        deps.discard(i)
        if dma_sem is None and eng == "pe":
            deps = {j for j in deps if not (self.ops[j]["eng"] == "pe" and self.ops[j]["dma_sem"] is None)}
        self.ops.append(dict(eng=eng, fn=fn, deps=deps, dma_sem=dma_sem))
        self.last_of_eng[eng] = i
        if dma_sem is not None:
            self.dma_ops.append(i)
        return i

    def barrier(self):
        deps = set(self.last_of_eng.values()) | set(self.dma_ops)
        self.dma_ops = []
        ids = []
        for eng in ENGS:
            ids.append(self.add(eng, lambda e: e.nop(), extra_deps=deps))
        return ids

    def emit(self, nc, block, stack):
        need = set()
        for op in self.ops:
            need |= op["deps"]
        sig = {}
        cnt = {e: 0 for e in ENGS}
        dcnt = {}
        sems = {}

        def getsem(name):
            if name not in sems:
                sems[name] = stack.enter_context(nc.semaphore(name))
            return sems[name]

        for i, op in enumerate(self.ops):
            if op["dma_sem"] is not None:
                k = op["dma_sem"]
                dcnt[k] = dcnt.get(k, 0) + 1
                sig[i] = (getsem("d_" + k), 16 * dcnt[k])
            elif i in need:
                c = cnt[op["eng"]]
                cnt[op["eng"]] = c + 1
                sig[i] = (getsem("e_%s_%d" % (op["eng"], c // EPOCH)), c % EPOCH + 1)
        by_eng = {e: [] for e in ENGS}
        for i, op in enumerate(self.ops):
            by_eng[op["eng"]].append(i)
        deco = dict(pe=block.tensor, act=block.scalar, dve=block.vector, pool=block.gpsimd, sp=block.sync)
        for engname in ENGS:
            def body(e, engname=engname):
                known = {}
                for i in by_eng[engname]:
                    op = self.ops[i]
                    waits = {}
                    for j in op["deps"]:
                        sem, val = sig[j]
                        key = id(sem)
                        if key not in waits or waits[key][1] < val:
                            waits[key] = (sem, val)
                    for key, (sem, val) in waits.items():
                        if known.get(key, 0) < val:
                            e.wait_ge(sem, val)
                            known[key] = val
                    ins = op["fn"](e)
                    if i in sig:
                        sem, _ = sig[i]
                        ins.then_inc(sem, 16 if op["dma_sem"] is not None else 1)
            deco[engname](body)


class SbufPool:
    def __init__(self, t, nbytes):
        self.t = t
        self.nbytes = nbytes
        self.off = 0

    def reset(self, off=0):
        self.off = off

    def alloc(self, free_shape, dtype):
        n = 1
        for s in free_shape:
            n *= s
        esz = 4 if dtype == F32 else 2
        nb = n * esz
        off = self.off
        self.off += (nb + 63) // 64 * 64
        assert self.off <= self.nbytes, ("SBUF pool overflow", self.off)
        ap = self.t[:, off // 2:(off + nb) // 2]
        if dtype != BF16:
            ap = ap.bitcast(dtype)
        if len(free_shape) == 2:
            ap = ap.rearrange("p (a b) -> p a b", a=free_shape[0])
        elif len(free_shape) == 3:
            ap = ap.rearrange("p (a b c) -> p a b c", a=free_shape[0], b=free_shape[1])
        return ap


class Ctx:
    pass


def seq(S, eng, fns, reads, writes):
    for fn in fns:
        S.add(eng, fn, reads=reads, writes=writes)


def make_common(nc, S, pool, ps):
    c = Ctx()
    c.nc, c.S, c.pool, c.ps = nc, S, pool, ps
    c.rr = {}
    return c


def rr(c, name, seq):
    i = c.rr.get(name, 0)
    c.rr[name] = i + 1
    return seq[i % len(seq)]


def wslot_load(c, parts):
    si = rr(c, "wslot", list(range(c.nslots)))
    slot = c.wslots[si]
    allkeys = [("w", si, 0), ("w", si, 1)]
    for pi, (view_fn, src) in enumerate(parts):
        dst = view_fn(slot)
        wk = [("w", si, pi)] if len(parts) == 2 else allkeys
        c.S.add("pool", lambda e, o=dst, i=src: e.dma_start(out=o, in_=i), writes=wk,
                dma_sem="w%d_%d" % (si, pi))
    return slot, allkeys


def rms_stats(c, s, gidx):
    S = c.S
    ssb = rr(c, "psC", [6, 7])
    ss_ps = c.ps[ssb]
    for kc in range(KD):
        sq = c.sq[kc % 2]
        xs = c.xT[:, kc, s * 512:(s + 1) * 512]
        S.add("act", lambda e, o=sq, i=xs: e.activation(o, i, AF.Square),
              reads=[("x", kc, s)], writes=[("sq", kc % 2)])
        S.add("pe", lambda e, o=ss_ps, l=c.ones, r=sq, st=(kc == 0), sp=(kc == KD - 1):
              e.matmul(o, l, r, start=st, stop=sp),
              reads=[("sq", kc % 2), "ones"], writes=[("ps", ssb)])
    rstd = c.rstd[s]

    S.add("act", lambda e, o=rstd, i=ss_ps: e.activation(o, i, AF.Sqrt, bias=c.epsr, scale=1.0 / D),
          reads=[("ps", ssb), "eps"], writes=[("rstd", s)])
    S.add("dve", lambda e, o=rstd: e.reciprocal(o, o), reads=[("rstd", s)], writes=[("rstd", s)])
    return rstd


def rms_norm_to_h(c, gidx):
    S = c.S
    for s in range(TT // 512):
        rstd = rms_stats(c, s, gidx)
        for kc in range(KD):
            xs = c.xT[:, kc, s * 512:(s + 1) * 512]
            hs = c.hT[:, kc, s * 512:(s + 1) * 512]
            gcol = c.gcols[:, gidx * KD + kc:gidx * KD + kc + 1]
            S.add("dve", lambda e, o=hs, i=xs, g=gcol, r=rstd:
                  e.scalar_tensor_tensor(o, i, g, r, ALU.mult, ALU.mult),
                  reads=[("x", kc, s), ("rstd", s), "gcols"], writes=[("h", kc, s)])


def big_a(c, f, s):
    return c.big_bf[:, f * TT + s * 512: f * TT + (s + 1) * 512], [("big", 2 * f + s)]


def big_f32(c, g, s):
    return c.big_f32[:, g * TT + s * 512: g * TT + (s + 1) * 512], [("big", 4 * g + 2 * s), ("big", 4 * g + 2 * s + 1)]


def big_uv(c, g, s):
    base = 8 * TT * 2
    return c.big_bf[:, base + g * TT + s * 512: base + g * TT + (s + 1) * 512], [("big", 32 + 2 * g + s)]


def ffn(c, gidx, wg, wu, wd):
    S = c.S
    rms_norm_to_h(c, gidx)
    wg_v = wg.rearrange("(kc p) f -> p kc f", p=128)
    wu_v = wu.rearrange("(kc p) f -> p kc f", p=128)
    wd_v = wd.rearrange("(fc p) d -> p fc d", p=128)
    NB = 11
    pend = []

    def issue_gu(b):
        f0 = b * 256
        gv = lambda sl: sl[:, 0:KD * 256].rearrange("p (k f) -> p k f", k=KD)
        uv = lambda sl: sl[:, KD * 256:2 * KD * 256].rearrange("p (k f) -> p k f", k=KD)
        return wslot_load(c, [(gv, wg_v[:, :, f0:f0 + 256]), (uv, wu_v[:, :, f0:f0 + 256])])

    def issue_d(b):
        d0 = b * 256
        dv = lambda sl: sl[:, 0:FC * 256].rearrange("p (k f) -> p k f", k=FC)
        return wslot_load(c, [(dv, wd_v[:, :, d0:d0 + 256])])

    loads = [("gu", b) for b in range(NB)] + [("d", b) for b in range(4)]
    PRE = c.nslots - 1
    issued = []

    def ensure(n):
        while len(issued) < min(n, len(loads)):
            kind, b = loads[len(issued)]
            issued.append(issue_gu(b) if kind == "gu" else issue_d(b))

    ensure(PRE)
    for b in range(NB):
        ensure(b + 1 + PRE)
        slot, keys = issued[b]
        for cc in range(2):
            f = 2 * b + cc
            for s in range(TT // 512):
                pg, pu = rr(c, "psA", [(0, 1), (2, 3)])
                hreads = [("h", kc, s) for kc in range(KD)]

                def mmg(e, o=c.ps[pg], slot=slot, cc=cc, s=s, off=0):
                    ins = None
                    for kc in range(KD):
                        l = slot[:, off + kc * 256 + cc * 128: off + kc * 256 + (cc + 1) * 128]
                        ins = e.matmul(o, l, c.hT[:, kc, s * 512:(s + 1) * 512], start=(kc == 0), stop=(kc == KD - 1))
                    return ins
                S.add("pe", mmg, reads=hreads + keys, writes=[("ps", pg)])
                S.add("pe", lambda e, o=c.ps[pu], slot=slot, cc=cc, s=s, f=mmg: f(e, o, slot, cc, s, KD * 256),
                      reads=hreads + keys, writes=[("ps", pu)])
                sil = c.tmpf[rr(c, "tmpf", [0, 1])]
                silk = ("tmpf", id(sil))
                S.add("act", lambda e, o=sil, i=c.ps[pg]: e.activation(o, i, AF.Silu),
                      reads=[("ps", pg)], writes=[silk])
                a_ap, a_keys = big_a(c, f, s)
                S.add("dve", lambda e, o=a_ap, a=sil, b_=c.ps[pu]: e.tensor_tensor(o, a, b_, ALU.mult),
                      reads=[silk, ("ps", pu)], writes=a_keys)
    for b in range(4):
        ensure(NB + b + 1 + PRE)
        slot, keys = issued[NB + b]
        for dc in range(2):
            d = 2 * b + dc
            for s in range(TT // 512):
                py = rr(c, "psB", [4, 5])
                areads = []
                for f in range(FC):
                    areads += big_a(c, f, s)[1]

                def mmd(e, o=c.ps[py], slot=slot, dc=dc, s=s):
                    ins = None
                    for f in range(FC):
                        l = slot[:, f * 256 + dc * 128: f * 256 + (dc + 1) * 128]
                        ins = e.matmul(o, l, big_a(c, f, s)[0], start=(f == 0), stop=(f == FC - 1))
                    return ins
                S.add("pe", mmd, reads=areads + keys, writes=[("ps", py)])
                xs = c.xT[:, d, s * 512:(s + 1) * 512]
                S.add("dve", lambda e, o=xs, i=c.ps[py]: e.scalar_tensor_tensor(o, i, 0.5, o, ALU.mult, ALU.add),
                      reads=[("ps", py), ("x", d, s)], writes=[("x", d, s)])


def sq_weight_blocks(c, w):
    wv = w.rearrange("(kc p) f -> p kc f", p=128)

    def load(j):
        v = lambda sl: sl[:, 0:KD * 512].rearrange("p (k f) -> p k f", k=KD)
        return wslot_load(c, [(v, wv[:, :, j * 512:(j + 1) * 512])])
    return load


def proj_feature_major(c, w, ncol_blocks, col0_block, rhs_fn, rhs_keys_fn, evac):
    S = c.S
    load = sq_weight_blocks(c, w)
    issued = [load(col0_block + j) for j in range(min(2, ncol_blocks))]
    for j in range(ncol_blocks):
        if j + 2 < ncol_blocks:
            issued.append(load(col0_block + j + 2))
        slot, keys = issued[j]
        for cc in range(4):
            oc = j * 4 + cc
            for s in range(TT // 512):
                py = rr(c, "psB", [4, 5])

                def mmf(e, o=c.ps[py], slot=slot, cc=cc, s=s):
                    ins = None
                    for kc in range(KD):
                        l = slot[:, kc * 512 + cc * 128: kc * 512 + (cc + 1) * 128]
                        ins = e.matmul(o, l, rhs_fn(kc, s), start=(kc == 0), stop=(kc == KD - 1))
                    return ins
                S.add("pe", mmf, reads=rhs_keys_fn(s) + keys, writes=[("ps", py)])
                evac(oc, s, py)


def h_rhs(c):
    return (lambda kc, s: c.hT[:, kc, s * 512:(s + 1) * 512]), (lambda s: [("h", kc, s) for kc in range(KD)])


def proj_token_major(c, w, col0_block, consume):
    S = c.S
    load = sq_weight_blocks(c, w)
    (slot0, k0), (slot1, k1) = load(col0_block), load(col0_block + 1)
    for tb in range(TT // 128):
        banks = rr(c, "psA", [(0, 1), (2, 3)])
        s = tb // 4
        for half, (slot, keys) in enumerate(((slot0, k0), (slot1, k1))):
            def mmt(e, o=c.ps[banks[half]], slot=slot, tb=tb):
                ins = None
                for kc in range(KD):
                    ins = e.matmul(o, c.hT[:, kc, tb * 128:(tb + 1) * 128], slot[:, kc * 512:(kc + 1) * 512],
                                   start=(kc == 0), stop=(kc == KD - 1))
                return ins
            S.add("pe", mmt, reads=[("h", kc, s) for kc in range(KD)] + keys, writes=[("ps", banks[half])])
        consume(tb, banks)


def gmlp(c, gidx, w_in, w_out):
    S = c.S
    rms_norm_to_h(c, gidx)

    def consume_v(tb, banks):
        vt = c.vtok[tb % 2]
        vk = ("vtok", tb % 2)
        for half in range(2):
            S.add("act", lambda e, o=vt[:, half * 512:(half + 1) * 512], i=c.ps[banks[half]]:
                  e.activation(o, i, AF.Gelu), reads=[("ps", banks[half])], writes=[(vk, half)])
        st = c.small[:, 0:12].rearrange("p (a b) -> p a b", a=2)
        mv = c.small[:, 12:14]
        rs = c.small[:, 14:15]

        seq(S, "dve", [lambda e, vt=vt: e.bn_stats(st[:, 0, :], vt[:, 0:512]),
                       lambda e, vt=vt: e.bn_stats(st[:, 1, :], vt[:, 512:1024]),
                       lambda e: e.bn_aggr(mv, st)], reads=[(vk, 0), (vk, 1)], writes=["small_mv"])
        S.add("act", lambda e: e.activation(rs, mv[:, 1:2], AF.Sqrt, bias=c.epsl, scale=1.0),
              reads=["small_mv", "eps"], writes=["small_rs"])

        seq(S, "dve", [lambda e: e.reciprocal(rs, rs),
                       lambda e, vt=vt: e.tensor_scalar(vt, vt, mv[:, 0:1], rs, ALU.subtract, ALU.mult)],
            reads=["small_mv", "small_rs", (vk, 0), (vk, 1)], writes=[(vk, 0), (vk, 1), "small_rs"])
        vn = c.vn[tb % 2]
        vnk = ("vn", tb % 2)

        seq(S, "dve", [lambda e, vt=vt: e.tensor_tensor(vt, vt, c.Gb, ALU.mult),
                       lambda e, vt=vt, vn=vn: e.tensor_tensor(vn, vt, c.Bb, ALU.add)],
            reads=[(vk, 0), (vk, 1), "Gb", "Bb"], writes=[(vk, 0), (vk, 1), vnk])
        sb = rr(c, "psB2", [(4, 5)])

        for hb in range(2):
            def mms(e, hb=hb, vn=vn, o=c.ps[sb[hb]]):
                ins = None
                for gg in range(4):
                    g = hb * 4 + gg
                    ins = e.matmul(o[:, gg * 128:(gg + 1) * 128], vn[:, g * 128:(g + 1) * 128], c.wsT[:, g, :],
                                   start=True, stop=True)
                return ins
            S.add("pe", mms, reads=[vnk, "wsT"], writes=[("ps", sb[hb])])
            o3 = c.big_f32[:, 0:8 * TT].rearrange("p (g t) -> p g t", g=8)[:, hb * 4:(hb + 1) * 4, tb * 128:(tb + 1) * 128]
            i3 = c.ps[sb[hb]].rearrange("p (g t) -> p g t", g=4)
            b3 = c.bsb[:, hb * 4:(hb + 1) * 4, :]
            s = tb // 4
            wk = []
            for g in range(hb * 4, hb * 4 + 4):
                wk += [("svt", g, tb)]
            S.add("dve", lambda e, o=o3, i=i3, b_=b3: e.tensor_tensor(o, i, b_, ALU.add),
                  reads=[("ps", sb[hb]), "bsb"], writes=wk + [("big", 4 * g + 2 * s + q) for g in range(hb * 4, hb * 4 + 4) for q in range(2)])

    proj_token_major(c, w_in, 2, consume_v)

    def evac_u(oc, s, py):
        gel = c.tmpf[rr(c, "tmpf", [0, 1])]
        gk = ("tmpf", id(gel))
        S.add("act", lambda e, o=gel, i=c.ps[py]: e.activation(o, i, AF.Gelu), reads=[("ps", py)], writes=[gk])
        sv_ap, sv_keys = big_f32(c, oc, s)
        uv_ap, uv_keys = big_uv(c, oc, s)
        S.add("dve", lambda e, o=uv_ap, a=gel, b_=sv_ap: e.tensor_tensor(o, a, b_, ALU.mult),
              reads=[gk] + sv_keys + [("svt", oc, tb) for tb in range(s * 4, s * 4 + 4)], writes=uv_keys)
    rf, rk = h_rhs(c)
    proj_feature_major(c, w_in, 2, 0, rf, rk, evac_u)

    def evac_o(oc, s, py):
        xs = c.xT[:, oc, s * 512:(s + 1) * 512]
        S.add("dve", lambda e, o=xs, i=c.ps[py]: e.tensor_tensor(o, i, o, ALU.add),
              reads=[("ps", py), ("x", oc, s)], writes=[("x", oc, s)])
    proj_feature_major(c, w_out, 2, 0, lambda kc, s: big_uv(c, kc, s)[0],
                       lambda s: [big_uv(c, kc, s)[1][0] for kc in range(KD)], evac_o)


def load_x_tile(c, src, t):
    for kc in range(KD):
        c.S.add("sp", lambda e, o=c.xT[:, kc, :], i=src[kc, :, t * TT:(t + 1) * TT]: e.dma_start(out=o, in_=i),
                writes=[("x", kc, 0), ("x", kc, 1)], dma_sem="x%d" % kc)


def store_x_tile(c, dst, t, tag):
    for kc in range(KD):
        c.S.add("sp", lambda e, i=c.xT[:, kc, :], o=dst[kc, :, t * TT:(t + 1) * TT]: e.dma_start(out=o, in_=i),
                reads=[("x", kc, 0), ("x", kc, 1)], writes=[(tag, kc, t)], dma_sem="st_%s" % tag)


def alloc_phase13(c):
    p = c.pool
    p.reset(0)
    c.xT = p.alloc((KD, TT), F32)
    c.hT = p.alloc((KD, TT), BF16)
    big = p.alloc((24 * TT,), BF16)
    c.big_bf = big
    c.big_f32 = big.bitcast(F32)
    c.nslots = 4
    c.wslots = [p.alloc((6144,), BF16) for _ in range(c.nslots)]
    c.sq = [p.alloc((512,), BF16) for _ in range(2)]
    c.rstd = [p.alloc((512,), F32) for _ in range(2)]
    c.tmpf = [p.alloc((512,), F32) for _ in range(2)]
    c.vtok = [p.alloc((1024,), F32) for _ in range(2)]
    c.vn = [p.alloc((1024,), BF16) for _ in range(2)]
    c.Gb = p.alloc((1024,), F32)
    c.Bb = p.alloc((1024,), F32)
    c.bsb = p.alloc((8, 128), F32)
    c.wsT = p.alloc((8, 128), BF16)
    c.wsT_f = p.alloc((8, 128), F32)
    c.maskf = p.alloc((128,), F32)
    c.gcols = p.alloc((64,), F32)
    c.ones = p.alloc((128,), BF16)
    c.small = p.alloc((16,), F32)
    epsb = p.alloc((2,), F32)
    c.epsr = epsb[:, 0:1]
    c.epsl = epsb[:, 1:2]
    c.stage = [p.alloc((1024,), BF16) for _ in range(2)]


def load_consts13(c, dr):
    S = c.S
    S.add("sp", lambda e: e.dma_start(out=c.gcols, in_=dr["gcols"]), writes=["gcols"], dma_sem="c0")
    seq(S, "dve", [lambda e: e.memset(c.ones, 1.0), lambda e: e.memset(c.epsr, RMS_EPS),
                   lambda e: e.memset(c.epsl, LN_EPS)], reads=[], writes=["ones", "eps"])


def load_consts_gmlp(c, dr):
    S = c.S
    S.add("sp", lambda e: e.dma_start(out=c.Gb, in_=dr["gm_ln_g"].partition_broadcast(128)[:, 0, :]), writes=["Gb"], dma_sem="c1")
    S.add("sp", lambda e: e.dma_start(out=c.Bb, in_=dr["gm_ln_b"].partition_broadcast(128)[:, 0, :]), writes=["Bb"], dma_sem="c2")
    S.add("sp", lambda e: e.dma_start(out=c.bsb.rearrange("p g t -> p (g t)"), in_=dr["gm_b_s"].partition_broadcast(128)[:, 0, :]),
          writes=["bsb"], dma_sem="c3")
    S.add("sp", lambda e: e.dma_start(out=c.wsT_f, in_=dr["gm_w_sT"]), writes=["wsT_f"], dma_sem="c4")
    S.add("sp", lambda e: e.dma_start(out=c.maskf, in_=dr["maskf"]), writes=["maskf"], dma_sem="c5")

    seq(S, "dve", [lambda e, g=g: e.tensor_tensor(c.wsT[:, g, :], c.wsT_f[:, g, :], c.maskf, ALU.mult) for g in range(8)],
        reads=["wsT_f", "maskf"], writes=["wsT"])


def phase1(c, dr, ex1, x1):
    S = c.S
    alloc_phase13(c)
    load_consts13(c, dr)
    load_consts_gmlp(c, dr)
    for t in range(NT):
        load_x_tile(c, dr["xT"], t)
        ffn(c, 0, dr["ffn1_w_gate"][0], dr["ffn1_w_up"][0], dr["ffn1_w_down"][0])
        gmlp(c, 1, dr["gm_w_in"][0], dr["gm_w_out"][0])
        ffn(c, 2, dr["ffn2_w_gate"][0], dr["ffn2_w_up"][0], dr["ffn2_w_down"][0])
        rms_norm_to_h(c, 3)

        def evac_k(oc, s, py, which=1, scale=1.0, t=t):
            st = c.stage[oc % 2]
            S.add("act", lambda e, o=st[:, s * 512:(s + 1) * 512], i=c.ps[py], sc=scale: e.activation(o, i, AF.Copy, scale=sc),
                  reads=[("ps", py)], writes=[("stage", oc % 2, s)])
            if s == 1:
                dst = ex1[oc, which, :].rearrange("(p n) -> p n", p=128)[:, t * TT:(t + 1) * TT]
                S.add("sp", lambda e, o=dst, i=st: e.dma_start(out=o, in_=i),
                      reads=[("stage", oc % 2, 0), ("stage", oc % 2, 1)], writes=[("ex1", which, oc, t)], dma_sem="st_ex1")
        rf, rk = h_rhs(c)
        proj_feature_major(c, dr["w_k"], 2, 0, rf, rk, evac_k)

        def consume_vv(tb, banks, t=t):
            st = c.vn[tb % 2]
            for half in range(2):
                S.add("act", lambda e, o=st[:, half * 512:(half + 1) * 512], i=c.ps[banks[half]]: e.activation(o, i, AF.Copy),
                      reads=[("ps", banks[half])], writes=[("vn", tb % 2)])
            tok0 = t * TT + tb * 128
            dst = ex1[:, 2, :].rearrange("h (n d) -> n h d", d=128)[tok0:tok0 + 128, :, :]
            S.add("sp", lambda e, o=dst, i=st.rearrange("p (h d) -> p h d", h=8): e.dma_start(out=o, in_=i),
                  reads=[("vn", tb % 2)], writes=[("ex1v", t, tb)], dma_sem="st_ex1")
        proj_token_major(c, dr["w_v"], 0, consume_vv)
        ffn(c, 4, dr["ffn1_w_gate"][1], dr["ffn1_w_up"][1], dr["ffn1_w_down"][1])
        rms_norm_to_h(c, 5)
        rf, rk = h_rhs(c)
        proj_feature_major(c, dr["da_w_q"][0], 2, 0, rf, rk,
                           lambda oc, s, py, t=t: evac_k(oc, s, py, which=0, scale=0.125, t=t))
        store_x_tile(c, x1, t, "x1")


def phase3(c, dr, ex2o, x1, outT):
    S = c.S
    alloc_phase13(c)
    load_consts13(c, dr)
    for t in range(NT):
        load_x_tile(c, x1, t)
        for h in range(8):
            S.add("sp", lambda e, o=c.hT[:, h, :], i=ex2o[h, :, t * TT:(t + 1) * TT]: e.dma_start(out=o, in_=i),
                  writes=[("h", h, 0), ("h", h, 1)], dma_sem="on%d" % h)

        def evac_o(oc, s, py):
            xs = c.xT[:, oc, s * 512:(s + 1) * 512]
            S.add("dve", lambda e, o=xs, i=c.ps[py]: e.tensor_tensor(o, i, o, ALU.add),
                  reads=[("ps", py), ("x", oc, s)], writes=[("x", oc, s)])
        rf, rk = h_rhs(c)
        proj_feature_major(c, dr["da_w_o"][0], 2, 0, rf, rk, evac_o)
        ffn(c, 6, dr["ffn2_w_gate"][1], dr["ffn2_w_up"][1], dr["ffn2_w_down"][1])
        for s in range(TT // 512):
            rstd = rms_stats(c, s, 7)
            for kc in range(KD):
                xs = c.xT[:, kc, s * 512:(s + 1) * 512]
                o_ap, o_keys = big_f32(c, kc, s)
                gcol = c.gcols[:, 7 * KD + kc:7 * KD + kc + 1]
                S.add("dve", lambda e, o=o_ap, i=xs, g=gcol, r=rstd: e.scalar_tensor_tensor(o, i, g, r, ALU.mult, ALU.mult),
                      reads=[("x", kc, s), ("rstd", s), "gcols"], writes=o_keys)
        for kc in range(KD):
            src = c.big_f32[:, kc * TT:(kc + 1) * TT]
            S.add("sp", lambda e, i=src, o=outT[kc, :, t * TT:(t + 1) * TT]: e.dma_start(out=o, in_=i),
                  reads=big_f32(c, kc, 0)[1] + big_f32(c, kc, 1)[1], writes=[("out", kc, t)], dma_sem="st_out")


def phase2(c, dr, ex1o, ex2):
    S = c.S
    p = c.pool
    p.reset(0)
    KA = p.alloc((SEQ,), BF16)
    KB = p.alloc((SEQ,), BF16)
    V = p.alloc((128, 128), BF16)
    QA = [p.alloc((512,), BF16) for _ in range(2)]
    QB = [p.alloc((512,), BF16) for _ in range(2)]
    NP = 3
    P1 = [p.alloc((512,), BF16) for _ in range(NP)]
    P2 = [p.alloc((512,), BF16) for _ in range(NP)]
    r1 = p.alloc((512,), F32)
    r2 = p.alloc((512,), F32)
    t1 = p.alloc((512,), F32)
    t2 = p.alloc((512,), F32)
    sqo = p.alloc((512,), BF16)
    rstd = p.alloc((512,), F32)
    onst = [p.alloc((512,), BF16) for _ in range(2)]
    btab = p.alloc((NDELTA,), F32)
    maskb = p.alloc((128,), BF16)
    ones = p.alloc((128,), BF16)
    lamv = p.alloc((4, 64), F32)
    lsm = p.alloc((8,), F32)
    subg = p.alloc((1,), F32)
    ps = c.ps

    S.add("sp", lambda e: e.dma_start(out=btab, in_=dr["btab"]), writes=["btab"], dma_sem="c0")
    S.add("sp", lambda e: e.dma_start(out=maskb, in_=dr["maskb"]), writes=["maskb"], dma_sem="c1")
    S.add("sp", lambda e: e.dma_start(out=subg, in_=dr["subg"]), writes=["subg"], dma_sem="c2")
    for i, nm in enumerate(["da_lam_q1", "da_lam_k1", "da_lam_q2", "da_lam_k2"]):
        S.add("sp", lambda e, i=i, nm=nm: e.dma_start(out=lamv[:, i, :], in_=dr[nm].partition_broadcast(128)[:, 0, :]),
              writes=[("lamv", i)], dma_sem="c3")
    for j in range(2):
        S.add("sp", lambda e, j=j: e.dma_start(out=QA[j][64:65, :], in_=dr["qaug"]), writes=[("QAaug", j)], dma_sem="c4")
        S.add("sp", lambda e, j=j: e.dma_start(out=QB[j][64:65, :], in_=dr["qaug"]), writes=[("QBaug", j)], dma_sem="c4")
    epsr = p.alloc((1,), F32)
    S.add("dve", lambda e: e.memset(epsr, RMS_EPS), writes=["eps"])
    S.add("dve", lambda e: e.memset(ones, 1.0), writes=["ones"])
    S.add("dve", lambda e: e.memset(KA[64:65, :], 1.0), writes=["KAaug"])
    S.add("dve", lambda e: e.memset(KB[64:65, :], 1.0), writes=["KBaug"])

    seq(S, "dve", [lambda e: e.tensor_tensor(lamv[:, 0, :], lamv[:, 0, :], lamv[:, 1, :], ALU.mult),
                   lambda e: e.tensor_tensor(lamv[:, 2, :], lamv[:, 2, :], lamv[:, 3, :], ALU.mult),
                   lambda e: e.reduce_sum(lsm[:, 0:1], lamv[:, 0, :], mybir.AxisListType.X),
                   lambda e: e.reduce_sum(lsm[:, 1:2], lamv[:, 2, :], mybir.AxisListType.X)],
        reads=[("lamv", i) for i in range(4)], writes=["lsm01", ("lamv", 0), ("lamv", 2)])
    S.add("act", lambda e: e.activation(lsm[:, 2:4], lsm[:, 0:2], AF.Exp), reads=["lsm01"], writes=["lsm23"])

    seq(S, "dve", [lambda e: e.tensor_tensor(lsm[:, 4:5], lsm[:, 3:4], lsm[:, 2:3], ALU.subtract),
                   lambda e: e.tensor_scalar(lsm[:, 5:6], lsm[:, 4:5], -LAMBDA_INIT, None, ALU.add)],
        reads=["lsm23"], writes=["neglam"])
    neglam = lsm[:, 5:6]

    unit = 0
    for b in range(2):
        for q in range(4):
            src = ex1o[b * 4 + q]
            kt = src[1, :].rearrange("(p n) -> p n", p=128)
            S.add("sp", lambda e, o=KA[0:64, q * 4096:(q + 1) * 4096], i=kt[0:64, :]: e.dma_start(out=o, in_=i),
                  writes=[("KA", q)], dma_sem="ka%d" % q)
            S.add("sp", lambda e, o=KB[0:64, q * 4096:(q + 1) * 4096], i=kt[64:128, :]: e.dma_start(out=o, in_=i),
                  writes=[("KB", q)], dma_sem="kb%d" % q)
            vt = src[2, :].rearrange("(n p d) -> p n d", p=128, d=128)
            S.add("sp", lambda e, o=V[:, q * 32:(q + 1) * 32, :], i=vt: e.dma_start(out=o, in_=i),
                  writes=[("V", q)], dma_sem="v%d" % q)
        for G in range(SEQ // 512):
            qi = G % 2
            srcq = ex1o[b * 4 + G // 8][0, :].rearrange("(p n) -> p n", p=128)
            off = (G % 8) * 512
            S.add("sp", lambda e, o=QA[qi][0:64, :], i=srcq[0:64, off:off + 512]: e.dma_start(out=o, in_=i),
                  writes=[("QA", qi)], dma_sem="qa%d" % qi)
            S.add("sp", lambda e, o=QB[qi][0:64, :], i=srcq[64:128, off:off + 512]: e.dma_start(out=o, in_=i),
                  writes=[("QB", qi)], dma_sem="qb%d" % qi)
            nJ = 4 * G + 4
            for J in range(nJ):
                jq0 = max(0, J - 4 * G)
                c0 = jq0 * 128
                sslot = unit % 2
                pslot = unit % NP
                unit += 1
                s1, s2 = ps[2 * sslot], ps[2 * sslot + 1]
                kq = J // 32
                S.add("pe", lambda e, o=s1[:, c0:512], l=KA[0:65, J * 128:(J + 1) * 128], r=QA[qi][0:65, c0:512]:
                      e.matmul(o, l, r, start=True, stop=True),
                      reads=[("KA", kq), "KAaug", ("QA", qi), ("QAaug", qi)], writes=[("ps", 2 * sslot)])
                S.add("pe", lambda e, o=s2[:, c0:512], l=KB[0:65, J * 128:(J + 1) * 128], r=QB[qi][0:65, c0:512]:
                      e.matmul(o, l, r, start=True, stop=True),
                      reads=[("KB", kq), "KBaug", ("QB", qi), ("QBaug", qi)], writes=[("ps", 2 * sslot + 1)])
                bcol = btab[:, 4 * G - J + 3: 4 * G - J + 4]
                S.add("act", lambda e, o=P1[pslot][:, c0:512], i=s1[:, c0:512], bb=bcol: e.activation(o, i, AF.Exp, bias=bb, scale=1.0),
                      reads=[("ps", 2 * sslot), "btab"], writes=[("P1", pslot)])
                S.add("act", lambda e, o=P2[pslot][:, c0:512], i=s2[:, c0:512], bb=bcol: e.activation(o, i, AF.Exp, bias=bb, scale=1.0),
                      reads=[("ps", 2 * sslot + 1), "btab"], writes=[("P2", pslot)])
                if J >= 4 * G:
                    S.add("dve", lambda e, a=P1[pslot][:, c0:c0 + 128]: e.tensor_tensor(a, a, maskb, ALU.mult),
                          reads=["maskb"], writes=[("P1", pslot)])
                    S.add("dve", lambda e, a=P2[pslot][:, c0:c0 + 128]: e.tensor_tensor(a, a, maskb, ALU.mult),
                          reads=["maskb"], writes=[("P2", pslot)])
                first, last = (J == 0), (J == nJ - 1)
                vblk = V[:, J, :]

                def av(e, c0=c0, pslot=pslot, vblk=vblk, first=first, last=last):
                    e.matmul(ps[4][:, c0:512], vblk, P1[pslot][:, c0:512], start=first, stop=last, skip_group_check=True)
                    e.matmul(ps[5][:, c0:512], vblk, P2[pslot][:, c0:512], start=first, stop=last, skip_group_check=True)
                    e.matmul(ps[6][:, c0:512], ones, P1[pslot][:, c0:512], start=first, stop=last, skip_group_check=True)
                    return e.matmul(ps[7][:, c0:512], ones, P2[pslot][:, c0:512], start=first, stop=last, skip_group_check=True)
                S.add("pe", av, reads=[("V", kq), ("P1", pslot), ("P2", pslot), "ones"],
                      writes=[("ps", 4), ("ps", 5), ("ps", 6), ("ps", 7)])
            S.add("dve", lambda e: e.reciprocal(r1, ps[6]), reads=[("ps", 6)], writes=["r1"])
            S.add("dve", lambda e: e.reciprocal(r2, ps[7]), reads=[("ps", 7)], writes=["r2"])
            S.add("dve", lambda e: e.tensor_tensor(t1, ps[4], r1, ALU.mult), reads=[("ps", 4), "r1"], writes=["t1"])
            S.add("dve", lambda e: e.tensor_tensor(t2, ps[5], r2, ALU.mult), reads=[("ps", 5), "r2"], writes=["t2"])
            S.add("dve", lambda e: e.scalar_tensor_tensor(t1, t2, neglam, t1, ALU.mult, ALU.add),
                  reads=["t1", "t2", "neglam"], writes=["t1"])
            S.add("act", lambda e: e.activation(sqo, t1, AF.Square), reads=["t1"], writes=["sqo"])
            sslot = unit % 2
            unit += 1
            ssp = ps[2 * sslot]
            S.add("pe", lambda e, o=ssp: e.matmul(o, ones, sqo, start=True, stop=True), reads=["sqo", "ones"], writes=[("ps", 2 * sslot)])
            oi = G % 2

            S.add("act", lambda e, ssp=ssp: e.activation(rstd, ssp, AF.Sqrt, bias=epsr, scale=1.0 / 128.0),
                  reads=[("ps", 2 * sslot), "eps"], writes=["rstd"])

            seq(S, "dve", [lambda e: e.reciprocal(rstd, rstd),
                           lambda e: e.scalar_tensor_tensor(t1, t1, subg[:, 0:1], rstd, ALU.mult, ALU.mult),
                           lambda e, oi=oi: e.tensor_single_scalar(onst[oi], t1, 1.0 - LAMBDA_INIT, ALU.mult)],
                reads=["rstd", "t1", "subg"], writes=["rstd", "t1", ("onst", oi)])
            dst = ex2[b * 4 + G // 8, :, (G % 8) * 512:(G % 8 + 1) * 512]
            S.add("sp", lambda e, o=dst, i=onst[oi]: e.dma_start(out=o, in_=i),
                  reads=[("onst", oi)], writes=[("ex2", b, G)], dma_sem="st_ex2")


W_NAMES = ["ffn1_w_gate", "ffn1_w_up", "ffn1_w_down", "ffn2_w_gate", "ffn2_w_up", "ffn2_w_down",
           "gm_w_in", "gm_w_out", "w_k", "w_v", "da_w_q", "da_w_o"]
W_SHAPES = {"ffn1_w_gate": [2, D, FF], "ffn1_w_up": [2, D, FF], "ffn1_w_down": [2, FF, D],
            "ffn2_w_gate": [2, D, FF], "ffn2_w_up": [2, D, FF], "ffn2_w_down": [2, FF, D],
            "gm_w_in": [1, D, 2 * D], "gm_w_out": [1, D, D], "w_k": [D, D], "w_v": [D, D],
            "da_w_q": [1, D, D], "da_w_o": [1, D, D]}
SBUF_BYTES = 190 * 1024


def build(mode):
    nc = bass.Bass("TRN2", target_bir_lowering=False)
    dr = {}

    def din(name, shape, dt=F32):
        dr[name] = nc.dram_tensor(name, shape, dt, kind="ExternalInput").ap()
        return dr[name]

    def dout(name, shape, dt=F32):
        dr[name] = nc.dram_tensor(name, shape, dt, kind="ExternalOutput").ap()
        return dr[name]

    need_w = {"A": ["ffn1_w_gate", "ffn1_w_up", "ffn1_w_down", "ffn2_w_gate", "ffn2_w_up", "ffn2_w_down",
                    "gm_w_in", "gm_w_out", "w_k", "w_v", "da_w_q"],
              "B": [],
              "C": ["ffn2_w_gate", "ffn2_w_up", "ffn2_w_down", "da_w_o"]}[mode]
    for n in need_w:
        din(n, W_SHAPES[n])
    if mode in ("A", "C"):
        din("gcols", [128, 64])
    if mode == "A":
        din("xT", [KD, 128, TCORE])
        din("gm_ln_g", [1, D]); din("gm_ln_b", [1, D]); din("gm_b_s", [1, D])
        din("gm_w_sT", [128, 8, 128]); din("maskf", [128, 128])
        dout("ex1", [8, 3, 128 * TCORE], BF16)
        dout("x1", [KD, 128, TCORE])
    if mode == "B":
        din("ex1o", [8, 3, 128 * TCORE], BF16)
        din("btab", [128, NDELTA]); din("maskb", [128, 128], BF16); din("subg", [128, 1])
        din("qaug", [1, 512], BF16)
        for nm in ["da_lam_q1", "da_lam_k1", "da_lam_q2", "da_lam_k2"]:
            din(nm, [1, 64])
        dout("ex2", [8, 128, TCORE], BF16)
    if mode == "C":
        din("x1", [KD, 128, TCORE])
        din("ex2o", [8, 128, TCORE], BF16)
        dout("outT", [KD, 128, TCORE])

    S = Sched()
    with ExitStack() as stack:
        sb = stack.enter_context(nc.sbuf_tensor("sbpool", [128, SBUF_BYTES // 2], BF16))
        ps = [stack.enter_context(nc.psum_tensor("ps%d" % i, [128, 512], F32)) for i in range(8)]
        c = make_common(nc, S, SbufPool(sb[:], SBUF_BYTES), [p[:] for p in ps])
        if mode == "A":
            phase1(c, dr, dr["ex1"], dr["x1"])
        elif mode == "B":
            phase2(c, dr, dr["ex1o"], dr["ex2"])
        elif mode == "C":
            phase3(c, dr, dr["ex2o"], dr["x1"], dr["outT"])
        S.barrier()
        block = stack.enter_context(nc.Block())
        S.emit(nc, block, stack)
    return nc


def _bf16(a):
    return a.astype(ml_dtypes.bfloat16)


def host_consts(inp):
    g = [inp["ffn_norm1_g"][0], inp["mix_norm_g"][0], inp["ffn_norm2_g"][0], inp["kv_norm_g"],
         inp["ffn_norm1_g"][1], inp["mix_norm_g"][1], inp["ffn_norm2_g"][1], inp["final_norm_g"]]
    gcols = np.stack([np.asarray(v, np.float32).reshape(KD, 128) for v in g], 0)
    gcols = np.ascontiguousarray(gcols.transpose(2, 0, 1).reshape(128, 64))
    rk = np.arange(128)
    maskf = (rk[:, None] <= rk[None, :]).astype(np.float32)
    return gcols, maskf


def kernel(**inp):
    inp = {k: np.asarray(v) for k, v in inp.items()}
    x = inp["x"].astype(np.float32, copy=False)
    gcols, maskf = host_consts(inp)
    cores = list(range(NCORE))
    wA = {n: np.ascontiguousarray(inp[n], dtype=np.float32) for n in W_NAMES}
    common13 = dict(gcols=gcols)
    inA = []
    for cidx in cores:
        b, q = cidx // 4, cidx % 4
        xT = np.ascontiguousarray(x[b, q * TCORE:(q + 1) * TCORE, :].T.reshape(KD, 128, TCORE))
        m = {n: wA[n] for n in ["ffn1_w_gate", "ffn1_w_up", "ffn1_w_down", "ffn2_w_gate", "ffn2_w_up", "ffn2_w_down",
                                "gm_w_in", "gm_w_out", "w_k", "w_v", "da_w_q"]}
        m.update(common13)
        m.update(xT=xT, gm_ln_g=inp["gm_ln_g"].reshape(1, D).astype(np.float32),
                 gm_ln_b=inp["gm_ln_b"].reshape(1, D).astype(np.float32),
                 gm_b_s=inp["gm_b_s"].reshape(1, D).astype(np.float32),
                 gm_w_sT=np.ascontiguousarray(inp["gm_w_s"][0].transpose(2, 0, 1).astype(np.float32)),
                 maskf=maskf)
        inA.append(m)
    resA = run_bass_kernel_spmd(build("A"), inA, core_ids=cores).results
    ex1 = np.stack([np.asarray(r["ex1"]) for r in resA], 0)
    ex1o = np.ascontiguousarray(ex1.transpose(1, 0, 2, 3))
    inB = []
    for h in cores:
        m_h = 2.0 ** (-(h + 1))
        rk = np.arange(128, dtype=np.float64)[:, None]
        dl = np.arange(NDELTA, dtype=np.float64)[None, :] - 3.0
        btab = (m_h * (rk - 128.0 * dl - 256.0)).astype(np.float32)
        qaug = _bf16((-m_h * (np.arange(512, dtype=np.float64) - 256.0)).reshape(1, 512).astype(np.float32))
        inB.append(dict(ex1o=ex1o[h], btab=btab, maskb=_bf16(maskf), qaug=qaug,
                        subg=inp["da_subln_g"].reshape(128, 1).astype(np.float32),
                        da_lam_q1=inp["da_lam_q1"].reshape(1, 64).astype(np.float32),
                        da_lam_k1=inp["da_lam_k1"].reshape(1, 64).astype(np.float32),
                        da_lam_q2=inp["da_lam_q2"].reshape(1, 64).astype(np.float32),
                        da_lam_k2=inp["da_lam_k2"].reshape(1, 64).astype(np.float32)))
    resB = run_bass_kernel_spmd(build("B"), inB, core_ids=cores).results
    ex2 = np.stack([np.asarray(r["ex2"]) for r in resB], 0)
    ex2o = np.ascontiguousarray(ex2.transpose(1, 0, 2, 3))
    inC = []
    for cidx in cores:
        m = {n: wA[n] for n in ["ffn2_w_gate", "ffn2_w_up", "ffn2_w_down", "da_w_o"]}
        m.update(common13)
        m.update(x1=np.asarray(resA[cidx]["x1"]), ex2o=ex2o[cidx])
        inC.append(m)
    resC = run_bass_kernel_spmd(build("C"), inC, core_ids=cores).results
    out = np.empty((2, SEQ, D), np.float32)
    for cidx in cores:
        b, q = cidx // 4, cidx % 4
        oT = np.asarray(resC[cidx]["outT"]).reshape(D, TCORE)
        out[b, q * TCORE:(q + 1) * TCORE, :] = oT.T
    return out
```
